# Optimizing a Trainium2 kernel written in Bass

```python
import math
import jax, jax.numpy as jnp
from jax import lax
import numpy as np

D_MODEL = 1024
BATCH = 2
SEQ = 8192
DEPTH = 2

N_HEADS = 16
HEAD_DIM = 64
N_KV_GROUPS = 4
HEADS_PER_GROUP = N_HEADS // N_KV_GROUPS
D_FF = 4 * D_MODEL
N_A_LAYERS = DEPTH // 2
N_B_LAYERS = DEPTH - N_A_LAYERS
CMP_BLOCK = 32
CMP_STRIDE = 16
CMP_HIDDEN = 4 * HEAD_DIM
SEL_BLOCK = 64
SEL_TOPN = 16
WINDOW = 512
NSA_Q_CHUNK = 64
MOBA_BLOCK = 256
MOBA_TOPK = 3
MOBA_Q_CHUNK = 32
REL_BUCKETS = 32
REL_MAX_DIST = 4096
RMS_EPS = 1e-6
NEG = -1e30
BIG = 1e9
NSA_IN_COLS = N_HEADS * HEAD_DIM + 6 * N_KV_GROUPS * HEAD_DIM + 3 * N_HEADS

kernel_name = "nsa_moba_yoco_hybrid"


def rms_norm(x, g):
    xf = x.astype(jnp.float32)
    y = xf * lax.rsqrt(jnp.mean(xf * xf, axis=-1, keepdims=True) + RMS_EPS)
    return (y * g.astype(jnp.float32)).astype(x.dtype)


def rel_bucket(dist):
    n = jnp.maximum(dist, 0)
    max_exact = REL_BUCKETS // 2
    nf = jnp.maximum(n, max_exact).astype(jnp.float32)
    large = max_exact + (jnp.log(nf / max_exact) / math.log(REL_MAX_DIST / max_exact)
                         * (REL_BUCKETS - max_exact)).astype(jnp.int32)
    large = jnp.minimum(large, REL_BUCKETS - 1)
    return jnp.where(n < max_exact, n, large)


def masked_softmax(logits, mask):
    p = jax.nn.softmax(jnp.where(mask, logits, NEG), axis=-1)
    return jnp.where(mask, p, 0.0)


def squared_relu_mlp(h, w_up, w_down):
    a = jax.nn.relu(h @ w_up)
    return (a * a) @ w_down


def nsa_mixer(h, w_in, q_gain, k_gain, cmp_pos, cmp_w1, cmp_w2, w_out, rel_table):
    B, S, _ = h.shape
    H, G, R, Dh = N_HEADS, N_KV_GROUPS, HEADS_PER_GROUP, HEAD_DIM
    scale = Dh ** -0.5
    proj = h @ w_in
    q = rms_norm(proj[..., :H * Dh].reshape(B, S, G, R, Dh), q_gain)
    kv = proj[..., H * Dh:H * Dh + 6 * G * Dh].reshape(B, S, 6, G, Dh)
    gates = jax.nn.sigmoid(proj[..., H * Dh + 6 * G * Dh:].astype(jnp.float32)).reshape(B, S, G, R, 3)
    k_cmp, v_cmp, k_sel, v_sel, k_win, v_win = [kv[:, :, i] for i in range(6)]

    n_cmp = (S - CMP_BLOCK) // CMP_STRIDE + 1
    cidx = np.arange(n_cmp)[:, None] * CMP_STRIDE + np.arange(CMP_BLOCK)[None, :]
    cmp_end = jnp.asarray(cidx[:, -1], jnp.int32)

    def compress(x_, pos, w1, w2):
        z = x_[:, cidx] + pos[None, None, :, None, :]
        hid = jax.nn.silu(jnp.einsum('bnlgd,ldf->bngf', z, w1.reshape(CMP_BLOCK, Dh, CMP_HIDDEN)))
        return jnp.einsum('bngf,fd->bngd', hid, w2)

    kc = rms_norm(compress(k_cmp, cmp_pos[0], cmp_w1[0], cmp_w2[0]), k_gain[0])
    vc = compress(v_cmp, cmp_pos[1], cmp_w1[1], cmp_w2[1])

    n_sel_blk = S // SEL_BLOCK
    cs = np.arange(n_cmp)[:, None] * CMP_STRIDE
    ss = np.arange(n_sel_blk)[None, :] * SEL_BLOCK
    overlap = np.clip(np.minimum(cs + CMP_BLOCK, ss + SEL_BLOCK) - np.maximum(cs, ss), 0, None) / CMP_BLOCK
    cmp_to_sel = jnp.asarray(overlap, jnp.float32)
    n_pick = min(SEL_TOPN, n_sel_blk)

    ks_blk = rms_norm(k_sel, k_gain[1]).reshape(B, n_sel_blk, SEL_BLOCK, G, Dh).transpose(0, 3, 1, 2, 4)
    vs_blk = v_sel.reshape(B, n_sel_blk, SEL_BLOCK, G, Dh).transpose(0, 3, 1, 2, 4)
    pad = ((0, 0), (WINDOW, 0), (0, 0), (0, 0))
    kw_pad = jnp.pad(rms_norm(k_win, k_gain[2]), pad)
    vw_pad = jnp.pad(v_win, pad)

    tb = rel_table.astype(jnp.float32).reshape(REL_BUCKETS, G, R)
    b_ix = jnp.arange(B)[:, None, None, None]
    g_ix = jnp.arange(G)[None, None, :, None]
    g_ix6 = jnp.arange(G)[None, None, :, None, None, None]
    r_ix6 = jnp.arange(R)[None, None, None, :, None, None]
    blk_ids = jnp.arange(n_sel_blk)
    Q = NSA_Q_CHUNK

    def chunk(s):
        t = s + jnp.arange(Q)
        qc = lax.dynamic_slice_in_dim(q, s, Q, axis=1)
        gc = lax.dynamic_slice_in_dim(gates, s, Q, axis=1)
        lc = jnp.einsum('bqgrd,bngd->bqgrn', qc, kc).astype(jnp.float32) * scale
        pc = masked_softmax(lc, (cmp_end[None, :] <= t[:, None])[None, :, None, None, :])
        oc = jnp.einsum('bqgrn,bngd->bqgrd', pc.astype(vc.dtype), vc)
        imp = jnp.einsum('bqgrn,nj->bqgj', pc, cmp_to_sel)
        cur = (t // SEL_BLOCK)[:, None]
        valid = blk_ids[None, :] <= cur
        forced = valid & ((blk_ids[None, :] == 0) | (blk_ids[None, :] >= cur - 1))
        score = jnp.where(forced[None, :, None, :], BIG, jnp.where(valid[None, :, None, :], imp, -BIG))
        _, sidx = lax.top_k(score, n_pick)
        kg = ks_blk[b_ix, g_ix, sidx]
        vg = vs_blk[b_ix, g_ix, sidx]
        kpos = sidx[..., None] * SEL_BLOCK + jnp.arange(SEL_BLOCK)
        dist = t[None, :, None, None, None] - kpos
        bias = tb[rel_bucket(dist)[:, :, :, None], g_ix6, r_ix6]
        ls = jnp.einsum('bqgrd,bqgnld->bqgrnl', qc, kg).astype(jnp.float32) * scale + bias
        ms = (dist >= 0)[:, :, :, None].reshape(B, Q, G, 1, -1)
        ps = masked_softmax(ls.reshape(B, Q, G, R, -1), ms).reshape(ls.shape)
        osel = jnp.einsum('bqgrnl,bqgnld->bqgrd', ps.astype(vg.dtype), vg)
        kw = lax.dynamic_slice_in_dim(kw_pad, s, WINDOW + Q, axis=1)
        vw = lax.dynamic_slice_in_dim(vw_pad, s, WINDOW + Q, axis=1)
        kp = s - WINDOW + jnp.arange(WINDOW + Q)
        dw = t[:, None] - kp[None, :]
        mw = (dw >= 0) & (dw < WINDOW) & (kp[None, :] >= 0)
        bw = tb[rel_bucket(dw)].transpose(0, 2, 3, 1)
        lw = jnp.einsum('bqgrd,bkgd->bqgrk', qc, kw).astype(jnp.float32) * scale + bw[None]
        pw = masked_softmax(lw, mw[None, :, None, None, :])
        ow = jnp.einsum('bqgrk,bkgd->bqgrd', pw.astype(vw.dtype), vw)
        o = gc[..., 0:1] * oc + gc[..., 1:2] * osel + gc[..., 2:3] * ow
        return o.reshape(B, Q, H * Dh).astype(h.dtype)

    starts = jnp.arange(S // Q, dtype=jnp.int32) * Q
    out = lax.map(chunk, starts)
    return jnp.moveaxis(out, 0, 1).reshape(B, S, H * Dh) @ w_out


def shared_kv(h, kv_norm, kv_w, k_gain):
    B, S, _ = h.shape
    G, Dh = N_KV_GROUPS, HEAD_DIM
    kv = (rms_norm(h, kv_norm) @ kv_w).reshape(B, S, 2, G, Dh)
    k = rms_norm(kv[:, :, 0], k_gain)
    v = kv[:, :, 1]
    n_blk = -(-S // MOBA_BLOCK)
    pad = ((0, 0), (0, n_blk * MOBA_BLOCK - S), (0, 0), (0, 0))
    k_blk = jnp.pad(k, pad).reshape(B, n_blk, MOBA_BLOCK, G, Dh).transpose(0, 3, 1, 2, 4)
    v_blk = jnp.pad(v, pad).reshape(B, n_blk, MOBA_BLOCK, G, Dh).transpose(0, 3, 1, 2, 4)
    k_mean = jnp.mean(k_blk.astype(jnp.float32), axis=3).astype(k.dtype)
    return k_blk, v_blk, k_mean


def moba_mixer(h, w_q, q_gain, w_out, k_blk, v_blk, k_mean, rel_table):
    B, S, _ = h.shape
    H, G, R, Dh = N_HEADS, N_KV_GROUPS, HEADS_PER_GROUP, HEAD_DIM
    L = MOBA_BLOCK
    scale = Dh ** -0.5
    q = rms_norm((h @ w_q).reshape(B, S, G, R, Dh), q_gain)
    n_blk = k_blk.shape[2]
    n_pick = min(MOBA_TOPK, n_blk)
    tb = rel_table.astype(jnp.float32).reshape(REL_BUCKETS, G, R)
    blk_ids = jnp.arange(n_blk)
    b_ix5 = jnp.arange(B)[:, None, None, None, None]
    g_ix5 = jnp.arange(G)[None, None, :, None, None]
    g_ix6 = jnp.arange(G)[None, None, :, None, None, None]
    r_ix6 = jnp.arange(R)[None, None, None, :, None, None]
    Q = MOBA_Q_CHUNK

    def chunk(s):
        t = s + jnp.arange(Q)
        cblk = s // L
        qc = lax.dynamic_slice_in_dim(q, s, Q, axis=1)
        gate = jnp.einsum('bqgrd,bgnd->bqgrn', qc, k_mean).astype(jnp.float32)
        gate = jnp.where(blk_ids < cblk, gate, NEG)
        _, bidx = lax.top_k(gate, n_pick)
        kg = k_blk[b_ix5, g_ix5, bidx]
        vg = v_blk[b_ix5, g_ix5, bidx]
        kpos = bidx[..., None] * L + jnp.arange(L)
        dist = t[None, :, None, None, None, None] - kpos
        bp = tb[rel_bucket(dist), g_ix6, r_ix6]
        lp = jnp.einsum('bqgrd,bqgrnld->bqgrnl', qc, kg).astype(jnp.float32) * scale + bp
        mp = jnp.broadcast_to((bidx < cblk)[..., None], lp.shape)
        ko = lax.dynamic_index_in_dim(k_blk, cblk, axis=2, keepdims=False)
        vo = lax.dynamic_index_in_dim(v_blk, cblk, axis=2, keepdims=False)
        do = t[:, None] - (cblk * L + jnp.arange(L))[None, :]
        bo = tb[rel_bucket(do)].transpose(0, 2, 3, 1)
        lo = jnp.einsum('bqgrd,bgld->bqgrl', qc, ko).astype(jnp.float32) * scale + bo[None]
        mo = jnp.broadcast_to((do >= 0)[None, :, None, None, :], lo.shape)
        logits = jnp.concatenate([lp.reshape(B, Q, G, R, -1), lo], axis=-1)
        mask = jnp.concatenate([mp.reshape(B, Q, G, R, -1), mo], axis=-1)
        p = masked_softmax(logits, mask)
        npast = n_pick * L
        pp = p[..., :npast].reshape(lp.shape)
        po = p[..., npast:]
        o = (jnp.einsum('bqgrnl,bqgrnld->bqgrd', pp.astype(vg.dtype), vg)
             + jnp.einsum('bqgrl,bgld->bqgrd', po.astype(vo.dtype), vo))
        return o.reshape(B, Q, H * Dh).astype(h.dtype)

    starts = jnp.arange(S // Q, dtype=jnp.int32) * Q
    out = lax.map(chunk, starts)
    return jnp.moveaxis(out, 0, 1).reshape(B, S, H * Dh) @ w_out


def setup_inputs(seed: int = 0) -> dict:
    key = jax.random.key(seed)
    ks = jax.random.split(key, 20)
    f32 = jnp.float32
    D, H, G, Dh = D_MODEL, N_HEADS, N_KV_GROUPS, HEAD_DIM

    def nrm(k, shape, scale):
        return jax.random.normal(k, shape, f32) * scale

    return {
        "x": nrm(ks[0], (BATCH, SEQ, D), 1.0),
        "norm_mix": 1.0 + nrm(ks[1], (DEPTH, D), 0.1),
        "norm_mlp": 1.0 + nrm(ks[2], (DEPTH, D), 0.1),
        "nsa_w_in": nrm(ks[3], (N_A_LAYERS, D, NSA_IN_COLS), D ** -0.5),
        "nsa_q_gain": 1.0 + nrm(ks[4], (N_A_LAYERS, Dh), 0.1),
        "nsa_k_gain": 1.0 + nrm(ks[5], (N_A_LAYERS, 3, Dh), 0.1),
        "nsa_cmp_pos": nrm(ks[6], (N_A_LAYERS, 2, CMP_BLOCK, Dh), 0.1),
        "nsa_cmp_w1": nrm(ks[7], (N_A_LAYERS, 2, CMP_BLOCK * Dh, CMP_HIDDEN), (CMP_BLOCK * Dh) ** -0.5),
        "nsa_cmp_w2": nrm(ks[8], (N_A_LAYERS, 2, CMP_HIDDEN, Dh), CMP_HIDDEN ** -0.5),
        "nsa_w_out": nrm(ks[9], (N_A_LAYERS, H * Dh, D), (H * Dh) ** -0.5),
        "kv_norm": 1.0 + nrm(ks[10], (D,), 0.1),
        "kv_w": nrm(ks[11], (D, 2 * G * Dh), D ** -0.5),
        "kv_k_gain": 1.0 + nrm(ks[12], (Dh,), 0.1),
        "moba_w_q": nrm(ks[13], (N_B_LAYERS, D, H * Dh), D ** -0.5),
        "moba_q_gain": 1.0 + nrm(ks[14], (N_B_LAYERS, Dh), 0.1),
        "moba_w_out": nrm(ks[15], (N_B_LAYERS, H * Dh, D), (H * Dh) ** -0.5),
        "rel_table": nrm(ks[16], (REL_BUCKETS, H), 0.2),
        "mlp_w_up": nrm(ks[17], (DEPTH, D, D_FF), D ** -0.5),
        "mlp_w_down": nrm(ks[18], (DEPTH, D_FF, D), D_FF ** -0.5),
    }


def reference(x, norm_mix, norm_mlp, nsa_w_in, nsa_q_gain, nsa_k_gain, nsa_cmp_pos, nsa_cmp_w1,
              nsa_cmp_w2, nsa_w_out, kv_norm, kv_w, kv_k_gain, moba_w_q, moba_q_gain, moba_w_out,
              rel_table, mlp_w_up, mlp_w_down):
    h = x
    for layer in range(DEPTH):
        hn = rms_norm(h, norm_mix[layer])
        if layer < N_A_LAYERS:
            i = layer
            h = h + nsa_mixer(hn, nsa_w_in[i], nsa_q_gain[i], nsa_k_gain[i], nsa_cmp_pos[i],
                              nsa_cmp_w1[i], nsa_cmp_w2[i], nsa_w_out[i], rel_table)
        else:
            if layer == N_A_LAYERS:
                k_blk, v_blk, k_mean = shared_kv(h, kv_norm, kv_w, kv_k_gain)
            j = layer - N_A_LAYERS
            h = h + moba_mixer(hn, moba_w_q[j], moba_q_gain[j], moba_w_out[j],
                               k_blk, v_blk, k_mean, rel_table)
        h = h + squared_relu_mlp(rms_norm(h, norm_mlp[layer]), mlp_w_up[layer], mlp_w_down[layer])
    return h
```

```python
import sys, time, contextlib, math
import numpy as np
import ml_dtypes
import numpy as np
import concourse.bass as bass
import concourse.mybir as mybir
from concourse.bass_utils import run_bass_kernel_spmd

F32 = mybir.dt.float32
BF16 = mybir.dt.bfloat16
ALU = mybir.AluOpType
AF = mybir.ActivationFunctionType
AX = mybir.AxisListType

SEM_LIM = 20000
N_DMA_SEMS = 24


class _Op:
    __slots__ = ("eng", "fn", "idx", "deps", "signal", "sigval", "dma", "dslot", "dval", "dprev")

    def __init__(self, eng, fn, dma):
        self.eng = eng
        self.fn = fn
        self.deps = set()
        self.signal = False
        self.dma = dma


class Sched:
    ENGS = ("pe", "act", "dve", "pool", "sp")

    def __init__(self, nc):
        self.nc = nc
        self.ops = {e: [] for e in self.ENGS}
        self.last_w = {}
        self.readers = {}

    def op(self, eng, fn, reads=(), writes=(), dma=False):
        o = _Op(eng, fn, dma)
        writes = list(writes) + [k for k in reads if isinstance(k, str) and k.startswith("bank")]
        reads = [k for k in reads if not (isinstance(k, str) and k.startswith("bank"))]
        for k in reads:
            w = self.last_w.get(k)
            if w is not None:
                o.deps.add(w)
        for k in writes:
            w = self.last_w.get(k)
            if w is not None:
                o.deps.add(w)
            for r in self.readers.get(k, ()):
                o.deps.add(r)
        for k in reads:
            self.readers.setdefault(k, []).append(o)
        for k in writes:
            self.last_w[k] = o
            self.readers[k] = []
        o.deps.discard(o)
        o.idx = len(self.ops[eng])
        self.ops[eng].append(o)
        return o

    def dma(self, eng, out, in_, reads=(), writes=(), **kw):
        return self.op(eng, lambda e: e.dma_start(out=out, in_=in_, **kw), reads, writes, dma=True)

    def emit(self, final_wait=True):
        nc = self.nc
        for e in self.ENGS:
            for o in self.ops[e]:
                nd = set()
                for d in o.deps:
                    if d.eng == "pe" and o.eng == "pe" and not d.dma:
                        continue
                    nd.add(d)
                o.deps = nd
                for d in nd:
                    d.signal = True
        counts = {}
        dcount = {}
        for e in self.ENGS:
            c = 0
            dc = 0
            for o in self.ops[e]:
                if o.dma:
                    o.dslot = dc % N_DMA_SEMS
                    o.dval = 16 * (dc // N_DMA_SEMS + 1)
                    dc += 1
                elif o.signal:
                    c += 1
                    o.sigval = c
            counts[e] = c
            dcount[e] = dc
        import contextlib
        with contextlib.ExitStack() as st:
            sems = {}
            for e in self.ENGS:
                n = (counts[e] + SEM_LIM - 1) // SEM_LIM
                sems[e] = [st.enter_context(nc.semaphore(f"s_{e}_{i}")) for i in range(n)]
            dsems = {}
            for e in self.ENGS:
                n = min(dcount[e], N_DMA_SEMS)
                dsems[e] = [st.enter_context(nc.semaphore(f"d_{e}_{i}")) for i in range(n)]
            block = st.enter_context(nc.Block())
            engmap = {"pe": "tensor", "act": "scalar", "dve": "vector", "pool": "gpsimd", "sp": "sync"}

            def body(ename, eng):
                waited = {}

                def wait_for(d):
                    if d.dma:
                        key = ("d", d.eng, d.dslot)
                        val = d.dval
                        sem = dsems[d.eng][d.dslot]
                    else:
                        ep = (d.sigval - 1) // SEM_LIM
                        key = ("c", d.eng, ep)
                        val = d.sigval - ep * SEM_LIM
                        sem = sems[d.eng][ep]
                    if waited.get(key, 0) >= val:
                        return
                    eng.wait_ge(sem, val)
                    waited[key] = val

                dq = []
                for o in self.ops[ename]:
                    for d in sorted(o.deps, key=lambda d: (d.eng, d.idx)):
                        wait_for(d)
                    if o.dma:
                        if o.dval > 16:
                            key = ("d", ename, o.dslot)
                            if waited.get(key, 0) < o.dval - 16:
                                eng.wait_ge(dsems[ename][o.dslot], o.dval - 16)
                                waited[key] = o.dval - 16
                        ins = o.fn(eng)
                        ins.then_inc(dsems[ename][o.dslot], 16)
                        dq.append(o)
                    else:
                        ins = o.fn(eng)
                        if o.signal:
                            ep = (o.sigval - 1) // SEM_LIM
                            ins.then_inc(sems[ename][ep], 1)
                if final_wait:
                    last = {}
                    for o in dq:
                        last[o.dslot] = o.dval
                    for slot, val in last.items():
                        key = ("d", ename, slot)
                        if waited.get(key, 0) < val:
                            eng.wait_ge(dsems[ename][slot], val)

            for ename in self.ENGS:
                if not self.ops[ename]:
                    continue
                deco = getattr(block, engmap[ename])

                def mk(ename=ename):
                    def f(eng):
                        body(ename, eng)
                    return f
                deco(mk())

BF = ml_dtypes.bfloat16
NEGM = -30000.0
NOFF = 24
HORD = [0, 2, 1, 3]


def rel_bucket_np(d):
    import jax, jax.numpy as jnp
    with jax.default_device(jax.devices("cpu")[0]):
        n = jnp.maximum(jnp.asarray(d, jnp.int32), 0)
        nf = jnp.maximum(n, 16).astype(jnp.float32)
        large = 16 + (jnp.log(nf / 16) / math.log(4096 / 16) * 16).astype(jnp.int32)
        large = jnp.minimum(large, 31)
        return np.asarray(jnp.where(n < 16, n, large))


def bias_onehots():
    L = NOFF * 128 + 256
    d = np.arange(L) - 127
    bk = rel_bucket_np(d)
    oh_s = np.zeros((33, L), np.float32)
    oh_w = np.zeros((33, L), np.float32)
    for arr, lim in ((oh_s, None), (oh_w, 512)):
        ok = d >= 0
        if lim is not None:
            ok = ok & (d < lim)
        arr[bk[ok], np.nonzero(ok)[0]] = 1.0
        arr[31, ok] -= 1.0
        arr[32, ~ok] = NEGM
    return oh_s, oh_w


def nsa_consts(S):
    NS = S // 64
    ncmp = (S - 32) // 16 + 1
    nct = (ncmp + 127) // 128
    c = {}
    c["idn"] = np.eye(128, dtype=np.float32)
    c["jmat"] = np.eye(128, dtype=np.float32)[::-1].copy().astype(BF)
    oh_s, oh_w = bias_onehots()
    c["oh_s"] = oh_s
    c["oh_w"] = oh_w
    nl = np.arange(128)[:, None]
    q = np.arange(128)[None, :]
    mc = np.zeros((128, 17, 2, 128), np.float32)
    for a in range(17):
        ok = (128 * a + q - 16 * nl - 31) >= 0
        mc[:, a, :, :] = np.where(ok, 0.0, NEGM)[:, None, :]
    c["maskc"] = mc.reshape(128, 17 * 256).astype(BF)
    cs = np.arange(nct * 128)[:, None] * 16
    ss = np.arange(NS)[None, :] * 64
    ov = np.clip(np.minimum(cs + 32, ss + 64) - np.maximum(cs, ss), 0, None) / 32.0
    ov[ncmp:] = 0
    c["c2s"] = ov.reshape(nct, 128, NS).transpose(1, 0, 2).reshape(128, nct * NS).astype(BF)
    nq = S // 128
    t = (np.arange(nq)[:, None] * 128 + np.arange(128)[None, :])
    cur = (t // 64)[:, :, None]
    j = np.arange(NS)[None, None, :]
    valid = j <= cur
    forced = valid & ((j == 0) | (j >= cur - 1))
    c["vm"] = (valid & ~forced).astype(np.float32)
    c["am"] = np.where(forced, 1e9, np.where(valid, 0.0, -1e9)).astype(np.float32)
    return c


def build_nsa(S, stop=None):
    NS = S // 64
    NQ = S // 128
    ncmp = (S - 32) // 16 + 1
    NCT = (ncmp + 127) // 128
    NCP = NCT * 128
    L = NOFF * 128 + 256
    nc = bass.Bass("TRN2", target_bir_lowering=False)
    D = nc.dram_tensor
    def inp(name, shape, dt=F32):
        return D(name, shape, dt, kind="ExternalInput").ap()
    x = inp("x", [S, 1024])
    gmix = inp("gmix", [1024])
    w1d = inp("w_tok1", [1024, 512])
    w2d = inp("w_tok2", [1024, 140])
    wfd = inp("w_feat", [1024, 256])
    gain1d = inp("gain1", [1, 512])
    posd = inp("cpos", [2, 2048])
    cw1d = inp("cw1", [2, 2048, 256])
    cw2kd = inp("cw2k", [256, 128])
    cw2vd = inp("cw2v", [256, 64])
    kg0d = inp("kg0", [1, 128])
    reld = inp("relg", [32, 4])
    idn = inp("idn", [128, 128])
    jmatd = inp("jmat", [128, 128], BF16)
    ohsd = inp("oh_s", [33, L])
    ohwd = inp("oh_w", [33, L])
    maskcd = inp("maskc", [128, 17 * 256], BF16)
    c2sd = inp("c2s", [128, NCT * NS], BF16)
    vmd = inp("vm", [NQ, 128, NS])
    amd = inp("am", [NQ, 128, NS])
    oout = D("o", [S, 256], F32, kind="ExternalOutput").ap()
    qtd = D("qtd", [NQ, 128, 256], BF16, kind="Internal").ap()
    fvd = D("fvd", [2, 4, L], F32, kind="Internal").ap()

    with contextlib.ExitStack() as st:
        NM = {}
        def T(name, shape, dt):
            t = st.enter_context(nc.sbuf_tensor("s_" + name, shape, dt))
            NM[id(t)] = name
            return t
        nm = lambda t: NM[id(t)]
        ps = st.enter_context(nc.psum_tensor("ps", [128, 8, 512], F32))
        psb = ps[:].bitcast(BF16)
        S_ = Sched(nc)
        marks = []
        def mark():
            marks.append({e: len(S_.ops[e]) for e in S_.ENGS})
        op = S_.op
        dma = S_.dma

        def bk(i):
            return f"bank{i}"

        idf = T("idf", [128, 128], F32)
        idb = T("idb", [128, 128], BF16)
        jmat = T("jmat", [128, 128], BF16)
        i2s = T("i2s", [128, 256], BF16)
        eps = T("eps", [128, 1], F32)
        gn = T("gn", [128, 8], F32)
        wt1 = T("wt1", [128, 8, 512], BF16)
        wt2 = T("wt2", [128, 8, 140], BF16)
        wft = T("wft", [128, 8, 256], BF16)
        gain1 = T("gain1", [128, 512], F32)
        kg0 = T("kg0", [128, 128], F32)
        KST = T("KST", [128, S], BF16)
        KWT = T("KWT", [128, S], BF16)
        VS = T("VS", [128, NQ, 80], BF16)
        VW = T("VW", [128, NQ, 80], BF16)
        G = T("G", [128, NQ, 12], F32)
        KC2 = T("KC2", [128, S + 48], BF16)
        VC2 = T("VC2", [128, S + 48], BF16)
        kcT = T("kcT", [128, NCP], BF16)
        VC = T("VC", [128, NCT, 80], BF16)
        bias_s = T("bias_s", [128, NOFF, 4, 128], BF16)
        bias_w = T("bias_w", [128, 5, 4, 128], BF16)
        maskc = T("maskc", [128, 17 * 256], BF16)
        c2s = T("c2s", [128, NCT * NS], BF16)
        xt = [T(f"stg{i}", [128, 1024], F32) for i in range(2)]
        stg = xt

        dma("sp", idf[:], idn[:, :], writes=["idf"])
        dma("sp", jmat[:], jmatd[:, :], writes=["jmat"])
        dma("sp", maskc[:], maskcd[:, :], writes=["maskc"])
        dma("sp", c2s[:], c2sd[:, :], writes=["c2s"])
        dma("sp", gn[:], gmix.rearrange("(c p) -> p c", p=128), writes=["gn"], allow_slow_non_contiguous=True)
        dma("sp", gain1[:], gain1d.partition_broadcast(128), writes=["gain1"])
        dma("sp", kg0[:], kg0d.partition_broadcast(128), writes=["kg0"])
        op("dve", lambda e: e.memset(eps[:], 1e-6), writes=["eps"])
        op("dve", lambda e: e.tensor_copy(out=idb[:], in_=idf[:]), reads=["idf"], writes=["idb"])
        op("dve", lambda e: e.tensor_scalar(out=i2s[:, 0:128], in0=idf[:], scalar1=-NEGM, scalar2=None, op0=ALU.mult), reads=["idf"], writes=["i2s"])
        op("dve", lambda e: e.tensor_scalar(out=i2s[:, 128:256], in0=idf[:], scalar1=-NEGM, scalar2=None, op0=ALU.mult), reads=["idf"], writes=["i2s"])
        op("dve", lambda e: e.tensor_scalar(out=gain1[:, 0:256], in0=gain1[:, 0:256], scalar1=0.125, scalar2=None, op0=ALU.mult), reads=["gain1"], writes=["gain1"])
        op("pool", lambda e: e.memset(KC2[:], 0.0), writes=["KC2"])
        op("pool", lambda e: e.memset(VC2[:], 0.0), writes=["VC2"])
        op("pool", lambda e: e.memset(VS[:, :, 64:65], 1.0), writes=["VS"])
        op("pool", lambda e: e.memset(VW[:, :, 64:65], 1.0), writes=["VW"])
        op("pool", lambda e: e.memset(VC[:, :, 64:65], 1.0), writes=["VC"])

        mark()
        for c in range(8):
            b = c % 2
            dma("sp", stg[b][:, 0:512], w1d[c * 128:(c + 1) * 128, :], writes=[f"stg{b}"])
            dma("sp", stg[b][:, 512:652], w2d[c * 128:(c + 1) * 128, :], writes=[f"stg{b}"])
            dma("sp", stg[b][:, 652:908], wfd[c * 128:(c + 1) * 128, :], writes=[f"stg{b}"])
            op("dve", lambda e, c=c, b=b: e.tensor_scalar(out=wt1[:, c, :], in0=stg[b][:, 0:512], scalar1=gn[:, c:c + 1], scalar2=None, op0=ALU.mult),
               reads=[f"stg{b}", "gn"], writes=["wt1"])
            op("dve", lambda e, c=c, b=b: e.tensor_scalar(out=wt2[:, c, :], in0=stg[b][:, 512:652], scalar1=gn[:, c:c + 1], scalar2=None, op0=ALU.mult),
               reads=[f"stg{b}", "gn"], writes=["wt2"])
            op("dve", lambda e, c=c, b=b: e.tensor_scalar(out=wft[:, c, :], in0=stg[b][:, 652:908], scalar1=gn[:, c:c + 1], scalar2=None, op0=ALU.mult),
               reads=[f"stg{b}", "gn"], writes=["wft"])

        mark()
        relt = T("relt", [33, 4], F32)
        oh = T("oh", [33, 512], F32)
        fv = T("fv", [4, 512], F32)
        op("dve", lambda e: e.memset(relt[:], 1.0), writes=["relt"])
        dma("sp", relt[0:32, :], reld[:, :], reads=[], writes=["relt"])
        for wi, (ohd, bt, noff) in enumerate(((ohsd, bias_s, NOFF), (ohwd, bias_w, 5))):
            for c0 in range(0, L, 512):
                c1 = min(L, c0 + 512)
                dma("sp", oh[:, 0:c1 - c0], ohd[:, c0:c1], writes=["oh"])
                op("pe", lambda e, c0=c0, c1=c1: e.matmul(ps[0:4, 0, 0:c1 - c0], lhsT=relt[:, :], rhs=oh[:, 0:c1 - c0], start=True, stop=True),
                   reads=["relt", "oh"], writes=[bk(0)])
                op("dve", lambda e, c0=c0, c1=c1: e.tensor_copy(out=fv[:, 0:c1 - c0], in_=ps[0:4, 0, 0:c1 - c0]), reads=[bk(0)], writes=["fv"])
                dma("sp", fvd[wi, :, c0:c1], fv[:, 0:c1 - c0], reads=["fv"], writes=["fvd"])
            for h in range(4):
                b = h % 2
                w = noff * 128
                for c0 in range(0, w, 1024):
                    c1 = min(w, c0 + 1024)
                    dma("sp", stg[b][:, 0:c1 - c0], bass.AP(fvd.tensor, fvd[wi, h, c0:c0 + 1].offset, [[1, 128], [1, c1 - c0]]),
                        reads=["fvd"], writes=[f"stg{b}"])
                    bi = HORD.index(h)
                    op("dve", lambda e, b=b, c0=c0, c1=c1, bi=bi, bt=bt: e.tensor_copy(
                        out=bt[:, c0 // 128:c1 // 128, bi, :], in_=stg[b][:, 0:c1 - c0].rearrange("p (o q) -> p o q", q=128)),
                       reads=[f"stg{b}"], writes=[nm(bt)])

        mark()
        sq = T("sq", [128, 1024], BF16)
        ss = T("ss", [128, 1], F32)
        rs = T("rs", [128, 1], F32)
        xn = T("xn", [128, 1024], BF16)
        hnT = T("hnT", [128, 8, 128], BF16)
        sq1 = T("sq1", [128, 512], F32)
        ssq = T("ssq", [128, 8], F32)
        rq = T("rq", [128, 8], F32)
        t1 = T("t1", [128, 512], F32)
        qk = [T(f"qk{i}", [128, 512], BF16) for i in range(2)]
        qTt = [T(f"qTt{i}", [128, 256], BF16) for i in range(2)]
        ge = T("ge", [128, 12], F32)

        def P1a(i):
            b = i % 2
            dma("sp", xt[b][:], x[i * 128:(i + 1) * 128, :], writes=[f"stg{b}"])
            op("act", lambda e: e.activation(out=sq[:], in_=xt[b][:], func=AF.Square, accum_out=ss[:]), reads=[f"stg{b}"], writes=["sq", "ss"])
            op("act", lambda e: e.activation(out=rs[:], in_=ss[:], func=AF.Sqrt, bias=eps[:], scale=1.0 / 1024), reads=["ss", "eps"], writes=["rs"])
            op("dve", lambda e: e.reciprocal(out=rs[:], in_=rs[:]), reads=["rs"], writes=["rs"])
            op("dve", lambda e: e.tensor_scalar(out=xn[:], in0=xt[b][:], scalar1=rs[:], scalar2=None, op0=ALU.mult), reads=[f"stg{b}", "rs"], writes=["xn"])
            for c in range(8):
                op("pe", lambda e, c=c: e.transpose(out=psb[:, 0, c * 128:(c + 1) * 128], in_=xn[:, c * 128:(c + 1) * 128], identity=idb[:]),
                   reads=["xn", "idb"], writes=[bk(0)])
            if i == 0: mark()
            op("act", lambda e: e.copy(out=hnT[:].rearrange("p a b -> p (a b)"), in_=psb[:, 0, 0:1024]), reads=[bk(0)], writes=["hnT"])
            if i == 0: mark()
            for c in range(8):
                op("pe", lambda e, c=c: e.matmul(ps[:, 1, :], lhsT=hnT[:, c, :], rhs=wt1[:, c, :], start=(c == 0), stop=(c == 7)),
                   reads=["hnT", "wt1"], writes=[bk(1)])
            for c in range(8):
                op("pe", lambda e, c=c: e.matmul(ps[:, 2, 0:140], lhsT=hnT[:, c, :], rhs=wt2[:, c, :], start=(c == 0), stop=(c == 7)),
                   reads=["hnT", "wt2"], writes=[bk(2)])
            for hh in range(2):
                for c in range(8):
                    op("pe", lambda e, c=c, hh=hh: e.matmul(ps[:, 3, hh * 128:(hh + 1) * 128], lhsT=wft[:, c, hh * 128:(hh + 1) * 128], rhs=hnT[:, c, :],
                                                            start=(c == 0), stop=(c == 7)), reads=["hnT", "wft"], writes=[bk(3)])
            if i == 0: mark()
            op("act", lambda e: e.activation(out=sq1[:], in_=ps[:, 1, :], func=AF.Square), reads=[bk(1)], writes=["sq1"])
            op("dve", lambda e: e.tensor_reduce(out=ssq[:], in_=sq1[:].rearrange("p (a d) -> p a d", d=64), axis=AX.X, op=ALU.add), reads=["sq1"], writes=["ssq"])
            op("act", lambda e: e.activation(out=rq[:], in_=ssq[:], func=AF.Sqrt, bias=eps[:], scale=1.0 / 64), reads=["ssq", "eps"], writes=["rq"])
            op("dve", lambda e: e.reciprocal(out=rq[:], in_=rq[:]), reads=["rq"], writes=["rq"])
            op("dve", lambda e: e.tensor_tensor(out=t1[:].rearrange("p (a d) -> p a d", d=64), in0=ps[:, 1, :].rearrange("p (a d) -> p a d", d=64),
                                                in1=rq[:].unsqueeze(2).broadcast_to([128, 8, 64]), op=ALU.mult), reads=[bk(1), "rq"], writes=["t1"])
            op("pool", lambda e: e.tensor_tensor(out=qk[b][:], in0=t1[:], in1=gain1[:], op=ALU.mult), reads=["t1", "gain1"], writes=[f"qk{b}"])
            if i == 0: mark()
            import os
            SK = os.environ.get("SKIP", "")
            if "a" not in SK:
                op("dve", lambda e: e.tensor_copy(out=VS[:, i, 0:64], in_=ps[:, 2, 0:64]), reads=[bk(2)], writes=["VS"])
            if "b" not in SK:
                op("dve", lambda e: e.tensor_copy(out=VW[:, i, 0:64], in_=ps[:, 2, 64:128]), reads=[bk(2)], writes=["VW"])
            if "c" not in SK:
                op("act", lambda e: e.activation(out=ge[:], in_=ps[:, 2, 128:140], func=AF.Exp, scale=-1.0), reads=[bk(2)], writes=["ge"])
            if "d" not in SK:
                op("dve", lambda e: e.tensor_scalar(out=ge[:], in0=ge[:], scalar1=1.0, scalar2=None, op0=ALU.add), reads=["ge"], writes=["ge"])
            if "e" not in SK:
                op("dve", lambda e: e.reciprocal(out=G[:, i, :], in_=ge[:]), reads=["ge"], writes=["G"])
            if i == 0: mark()
            t0 = i * 128
            for (dst, c0) in ((KC2, 0), (VC2, 128)):
                op("act", lambda e, dst=dst, c0=c0: e.copy(out=dst[0:64, 16 + t0:16 + t0 + 128], in_=ps[0:64, 3, c0:c0 + 128]), reads=[bk(3)], writes=[nm(dst)])
                op("act", lambda e, dst=dst, c0=c0: e.copy(out=dst[64:128, 15 + t0:15 + t0 + 128], in_=ps[64:128, 3, c0:c0 + 128]), reads=[bk(3)], writes=[nm(dst)])

        def P1b(i):
            b = i % 2
            if i == 0: mark()
            for j in range(4):
                op("pe", lambda e, j=j: e.transpose(out=psb[:, 4, j * 128:(j + 1) * 128], in_=qk[b][:, j * 128:(j + 1) * 128], identity=idb[:]),
                   reads=[f"qk{b}", "idb"], writes=[bk(4)])
            op("act", lambda e: e.copy(out=qTt[b][:], in_=psb[:, 4, 0:256]), reads=[bk(4)], writes=[f"qTt{b}"])
            op("dve", lambda e: e.tensor_copy(out=KST[:, i * 128:(i + 1) * 128], in_=psb[:, 4, 256:384]), reads=[bk(4)], writes=["KST"])
            op("dve", lambda e: e.tensor_copy(out=KWT[:, i * 128:(i + 1) * 128], in_=psb[:, 4, 384:512]), reads=[bk(4)], writes=["KWT"])
            dma("sp", qtd[i], qTt[b][:], reads=[f"qTt{b}"], writes=["qtd"])

        P1a(0)
        for i in range(NQ):
            if i + 1 < NQ:
                P1a(i + 1)
            P1b(i)

        mark()
        cw1 = T("cw1", [128, 16, 256], BF16)
        posb = T("posb", [128, 16], BF16)
        posf = T("posf", [128, 16], F32)
        cb = T("cb", [128, 2], F32)
        hidT = T("hidT", [128, 2, NCP], BF16)
        cw2k = T("cw2k", [128, 2, 128], BF16)
        cw2v = T("cw2v", [128, 2, 64], BF16)
        kn = T("kn", [128, 128], BF16)
        for c in range(2):
            dma("sp", stg[0][:, c * 128:(c + 1) * 128], cw2kd[c * 128:(c + 1) * 128, :], writes=["stg0"])
            dma("sp", stg[0][:, 256 + c * 64:256 + (c + 1) * 64], cw2vd[c * 128:(c + 1) * 128, :], writes=["stg0"])
        op("dve", lambda e: e.tensor_copy(out=cw2k[:].rearrange("p a b -> p (a b)"), in_=stg[0][:, 0:256]), reads=["stg0"], writes=["cw2k"])
        op("dve", lambda e: e.tensor_copy(out=cw2v[:].rearrange("p a b -> p (a b)"), in_=stg[0][:, 256:384]), reads=["stg0"], writes=["cw2v"])
        for kv in range(2):
            src = KC2 if kv == 0 else VC2
            for j4 in range(0, 16, 4):
                dma("sp", posf[:, j4:j4 + 4], posd[kv, j4 * 128:(j4 + 4) * 128].rearrange("(j p) -> p j", p=128), writes=["posf"], allow_slow_non_contiguous=True)
            op("dve", lambda e: e.tensor_copy(out=posb[:], in_=posf[:]), reads=["posf"], writes=["posb"])
            for jj in range(0, 16, 4):
                b = (jj // 4) % 2
                dma("sp", stg[b][:].rearrange("p (j f) -> p j f", f=256), cw1d[kv, jj * 128:(jj + 4) * 128, :].rearrange("(j p) f -> p j f", p=128), writes=[f"stg{b}"])
                op("dve", lambda e, jj=jj, b=b: e.tensor_copy(out=cw1[:, jj:jj + 4, :], in_=stg[b][:].rearrange("p (j f) -> p j f", f=256)), reads=[f"stg{b}"], writes=["cw1"])
            for fc in range(2):
                for j in range(16):
                    op("pe", lambda e, j=j, fc=fc: e.matmul(ps[:, 5, fc:fc + 1], lhsT=cw1[:, j, fc * 128:(fc + 1) * 128], rhs=posb[:, j:j + 1],
                                                            start=(j == 0), stop=(j == 15)), reads=["cw1", "posb"], writes=[bk(5)])
            op("dve", lambda e: e.tensor_copy(out=cb[:], in_=ps[:, 5, 0:2]), reads=[bk(5)], writes=["cb"])
            for fc in range(2):
                for n0 in range(0, ncmp, 512):
                    n1 = min(ncmp, n0 + 512)
                    for j in range(16):
                        c0 = 16 + 16 * n0 + 2 * j
                        op("pe", lambda e, j=j, fc=fc, c0=c0, n0=n0, n1=n1, src=src: e.matmul(
                            ps[:, 6, 0:n1 - n0], lhsT=cw1[:, j, fc * 128:(fc + 1) * 128],
                            rhs=src[:, c0:c0 + 16 * (n1 - n0 - 1) + 1:16], start=(j == 0), stop=(j == 15)),
                           reads=["cw1", nm(src)], writes=[bk(6)])
                    op("act", lambda e, fc=fc, n0=n0, n1=n1: e.activation(out=hidT[:, fc, n0:n1], in_=ps[:, 6, 0:n1 - n0], func=AF.Silu, bias=cb[:, fc:fc + 1]),
                       reads=[bk(6), "cb"], writes=["hidT"])
            if ncmp < NCP:
                op("pool", lambda e: e.memset(hidT[:, :, ncmp:NCP], 0.0), writes=["hidT"])
            for m in range(NCT):
                if kv == 0:
                    for fc in range(2):
                        op("pe", lambda e, fc=fc, m=m: e.matmul(ps[:, 7, 0:128], lhsT=hidT[:, fc, m * 128:(m + 1) * 128], rhs=cw2k[:, fc, :],
                                                                start=(fc == 0), stop=(fc == 1)), reads=["hidT", "cw2k"], writes=[bk(7)])
                    op("act", lambda e: e.activation(out=sq1[:, 0:128], in_=ps[:, 7, 0:128], func=AF.Square), reads=[bk(7)], writes=["sq1"])
                    op("dve", lambda e: e.tensor_reduce(out=ssq[:, 0:2], in_=sq1[:, 0:128].rearrange("p (a d) -> p a d", d=64), axis=AX.X, op=ALU.add),
                       reads=["sq1"], writes=["ssq"])
                    op("act", lambda e: e.activation(out=rq[:, 0:2], in_=ssq[:, 0:2], func=AF.Sqrt, bias=eps[:], scale=1.0 / 64), reads=["ssq", "eps"], writes=["rq"])
                    op("dve", lambda e: e.reciprocal(out=rq[:, 0:2], in_=rq[:, 0:2]), reads=["rq"], writes=["rq"])
                    op("dve", lambda e: e.tensor_tensor(out=t1[:, 0:128].rearrange("p (a d) -> p a d", d=64), in0=ps[:, 7, 0:128].rearrange("p (a d) -> p a d", d=64),
                                                        in1=rq[:, 0:2].unsqueeze(2).broadcast_to([128, 2, 64]), op=ALU.mult), reads=[bk(7), "rq"], writes=["t1"])
                    op("pool", lambda e: e.tensor_tensor(out=kn[:], in0=t1[:, 0:128], in1=kg0[:], op=ALU.mult), reads=["t1", "kg0"], writes=["kn"])
                    op("pe", lambda e: e.transpose(out=psb[:, 5, 0:128], in_=kn[:], identity=idb[:]), reads=["kn", "idb"], writes=[bk(5)])
                    op("act", lambda e, m=m: e.copy(out=kcT[:, m * 128:(m + 1) * 128], in_=psb[:, 5, 0:128]), reads=[bk(5)], writes=["kcT"])
                else:
                    for fc in range(2):
                        op("pe", lambda e, fc=fc, m=m: e.matmul(ps[:, 7, 0:64], lhsT=hidT[:, fc, m * 128:(m + 1) * 128], rhs=cw2v[:, fc, :],
                                                                start=(fc == 0), stop=(fc == 1)), reads=["hidT", "cw2v"], writes=[bk(7)])
                    op("dve", lambda e, m=m: e.tensor_copy(out=VC[:, m, 0:64], in_=ps[:, 7, 0:64]), reads=[bk(7)], writes=["VC"])

        mark()
        qT = [T(f"qT{i}", [128, 2, 128], BF16) for i in range(2)]
        vmt = [T(f"vmt{i}", [128, NS], F32) for i in range(2)]
        amt = [T(f"amt{i}", [128, NS], F32) for i in range(2)]
        PT = [T(f"PT{i}", [128, 512], BF16) for i in range(3)]
        Oe = T("Oe", [65, 512], F32)
        den = T("den", [128, 4], F32)
        wg = T("wg", [128, 4], F32)
        rdc = T("rdc", [128, 4], F32)
        oacc = [T(f"oacc{i}", [128, 256], F32) for i in range(2)]
        imp = T("imp", [128, NS], F32)
        score = T("score", [128, NS], F32)
        score2 = T("score2", [128, NS], F32)
        m8a = T("m8a", [128, 8], F32)
        m8b = T("m8b", [128, 8], F32)
        nsel = T("nsel", [128, NS], BF16)
        nselx = KC2
        cnt = {"st": 0, "pt": 0}

        def attn_tiles(i, qb, tiles, Vt, obank, extra=None):
            nt = len(tiles)
            for ti, (kt_ap, kt_keys, vidx, adds) in enumerate(tiles):
                sb = 2 * (cnt["st"] % 2)
                cnt["st"] += 1
                pi = cnt["pt"] % 3
                cnt["pt"] += 1
                nadd = len(adds)
                for half in range(2):
                    rows = slice(64 * half, 64 * half + 64)
                    op("pe", lambda e, half=half, rows=rows, sb=sb, kt_ap=kt_ap, nadd=nadd: e.matmul(
                        ps[:, sb + half, 0:256], lhsT=kt_ap[rows, :], rhs=qT[qb][rows, :, :], start=True, stop=(nadd == 0)),
                       reads=kt_keys + [f"qT{qb}"], writes=[bk(sb + half)])
                for ai, (l_ap, r_ap, keys) in enumerate(adds):
                    for half in range(2):
                        op("pe", lambda e, half=half, sb=sb, l_ap=l_ap, r_ap=r_ap, ai=ai, nadd=nadd: e.matmul(
                            ps[:, sb + half, 0:256], lhsT=l_ap, rhs=r_ap[half], start=False, stop=(ai == nadd - 1)),
                           reads=keys, writes=[bk(sb + half)])
                op("act", lambda e, sb=sb, pi=pi: e.activation(out=PT[pi][:].rearrange("p (a b) -> p a b", a=2), in_=ps[:, sb:sb + 2, 0:256], func=AF.Exp),
                   reads=[bk(sb), bk(sb + 1)], writes=[f"PT{pi}"])
                op("pe", lambda e, pi=pi, ti=ti, vidx=vidx: e.matmul(ps[0:65, obank, :], lhsT=Vt[:, vidx, 0:65], rhs=PT[pi][:], start=(ti == 0), stop=(ti == nt - 1)),
                   reads=[f"PT{pi}", nm(Vt)], writes=[bk(obank)])
                if extra is not None:
                    extra(ti, nt, pi)

        def finalize(i, br, obank, first, keep_rden=False):
            ob = i % 2
            op("act", lambda e: e.copy(out=Oe[:], in_=ps[0:65, obank, :]), reads=[bk(obank)], writes=["Oe"])
            for bi in range(4):
                op("pe", lambda e, bi=bi: e.transpose(out=ps[:, 7, bi * 68:bi * 68 + 65], in_=Oe[0:65, bi * 128:(bi + 1) * 128], identity=idf[0:65, 0:65]),
                   reads=["Oe", "idf"], writes=[bk(7)])
            pv = ps[:, 7, 0:272].rearrange("p (a d) -> p a d", d=68)
            op("dve", lambda e: e.tensor_scalar(out=den[:], in0=pv[:, :, 64], scalar1=1e-30, scalar2=None, op0=ALU.max), reads=[bk(7)], writes=["den"])
            dst = rdc if keep_rden else den
            op("dve", lambda e: e.reciprocal(out=dst[:], in_=den[:]), reads=["den"], writes=[nm(dst)])
            Gv = G[:, i, :].rearrange("p (r b) -> p r b", b=3)
            op("dve", lambda e: e.tensor_tensor(out=wg[:, 0:2], in0=dst[:, 0:2], in1=Gv[:, 0:4:2, br], op=ALU.mult), reads=[nm(dst), "G"], writes=["wg"])
            op("dve", lambda e: e.tensor_tensor(out=wg[:, 2:4], in0=dst[:, 2:4], in1=Gv[:, 1:4:2, br], op=ALU.mult), reads=[nm(dst), "G"], writes=["wg"])
            for bi in range(4):
                h = HORD[bi]
                if first:
                    op("dve", lambda e, bi=bi, h=h: e.tensor_scalar(out=oacc[ob][:, h * 64:(h + 1) * 64], in0=pv[:, bi, 0:64], scalar1=wg[:, bi:bi + 1],
                                                                    scalar2=None, op0=ALU.mult), reads=[bk(7), "wg"], writes=[f"oacc{ob}"])
                else:
                    op("dve", lambda e, bi=bi, h=h: e.scalar_tensor_tensor(out=oacc[ob][:, h * 64:(h + 1) * 64], in0=pv[:, bi, 0:64], scalar=wg[:, bi:bi + 1],
                                                                           in1=oacc[ob][:, h * 64:(h + 1) * 64], op0=ALU.mult, op1=ALU.add),
                       reads=[bk(7), "wg", f"oacc{ob}"], writes=[f"oacc{ob}"])

        i2h = [i2s[:, :], i2s[:, :]]
        def qtile(i):
                qb = i % 2
                dma("sp", qT[qb][:].rearrange("p a b -> p (a b)"), qtd[i], reads=["qtd"], writes=[f"qT{qb}"])
                dma("sp", vmt[qb][:], vmd[i], writes=[f"vmt{qb}"])
                dma("sp", amt[qb][:], amd[i], writes=[f"amt{qb}"])
                M_i = (8 * i + 7 + 127) // 128
                tiles = []
                for m in range(M_i):
                    a = i - 16 * m
                    adds = []
                    if a <= 16:
                        mt = maskc[:, a * 256:(a + 1) * 256]
                        adds.append((idb[:, :], [mt, mt], ["idb", "maskc"]))
                    tiles.append((kcT[:, m * 128:(m + 1) * 128], ["kcT"], m, adds))

                def extra_c(ti, nt, pi):
                    for bi in range(4):
                        op("pe", lambda e, bi=bi, ti=ti, pi=pi: e.matmul(ps[:, 5, bi * NS:(bi + 1) * NS], lhsT=PT[pi][:, bi * 128:(bi + 1) * 128],
                                                                         rhs=c2s[:, ti * NS:(ti + 1) * NS], start=(ti == 0 and bi == 0), stop=(ti == nt - 1 and bi == 3)),
                           reads=[f"PT{pi}", "c2s"], writes=[bk(5)])
                attn_tiles(i, qb, tiles, VC, 4, extra=extra_c)
                finalize(i, 0, 4, True, keep_rden=True)
                for bi in range(4):
                    if bi == 0:
                        op("dve", lambda e: e.tensor_scalar(out=imp[:], in0=ps[:, 5, 0:NS], scalar1=rdc[:, 0:1], scalar2=None, op0=ALU.mult),
                           reads=[bk(5), "rdc"], writes=["imp"])
                    else:
                        op("dve", lambda e, bi=bi: e.scalar_tensor_tensor(out=imp[:], in0=ps[:, 5, bi * NS:(bi + 1) * NS], scalar=rdc[:, bi:bi + 1], in1=imp[:],
                                                                          op0=ALU.mult, op1=ALU.add), reads=[bk(5), "rdc", "imp"], writes=["imp"])
                op("pool", lambda e: e.tensor_tensor(out=score[:], in0=imp[:], in1=vmt[qb][:], op=ALU.mult), reads=["imp", f"vmt{qb}"], writes=["score"])
                op("pool", lambda e: e.tensor_tensor(out=score[:], in0=score[:], in1=amt[qb][:], op=ALU.add), reads=["score", f"amt{qb}"], writes=["score"])
                op("dve", lambda e: e.max(out=m8a[:], in_=score[:]), reads=["score"], writes=["m8a"])
                op("dve", lambda e: e.match_replace(out=score2[:], in_to_replace=m8a[:], in_values=score[:], imm_value=-3e9), reads=["score", "m8a"], writes=["score2"])
                op("dve", lambda e: e.max(out=m8b[:], in_=score2[:]), reads=["score2"], writes=["m8b"])
                op("dve", lambda e: e.tensor_scalar(out=nsel[:], in0=score[:], scalar1=m8b[:, 7:8], scalar2=1.0, op0=ALU.is_ge, op1=ALU.subtract),
                   reads=["score", "m8b"], writes=["nsel"])
                nb = 2 * (i + 1)
                op("pool", lambda e, nb=nb: e.tensor_copy(out=nselx[:, 0:nb * 64].rearrange("p (j k) -> p j k", k=64),
                                                          in_=nsel[:, 0:nb].unsqueeze(2).broadcast_to([128, nb, 64])), reads=["nsel"], writes=["KC2"])
                tiles = []
                for kt in range(i + 1):
                    o_ = i - kt
                    adds = []
                    if o_ < NOFF:
                        adds.append((jmat[:, :], [bias_s[:, o_, 0:2, :], bias_s[:, o_, 2:4, :]], ["jmat", "bias_s"]))
                    adds.append((nselx[:, kt * 128:(kt + 1) * 128], i2h, ["KC2", "i2s"]))
                    tiles.append((KST[:, kt * 128:(kt + 1) * 128], ["KST"], kt, adds))
                attn_tiles(i, qb, tiles, VS, 6)
                finalize(i, 1, 6, False)
                tiles = []
                for kt in range(max(0, i - 4), i + 1):
                    o_ = i - kt
                    adds = [(jmat[:, :], [bias_w[:, o_, 0:2, :], bias_w[:, o_, 2:4, :]], ["jmat", "bias_w"])]
                    tiles.append((KWT[:, kt * 128:(kt + 1) * 128], ["KWT"], kt, adds))
                attn_tiles(i, qb, tiles, VW, 4)
                finalize(i, 2, 4, False)
                dma("sp", oout[i * 128:(i + 1) * 128, :], oacc[i % 2][:], reads=[f"oacc{i % 2}"])

        import os
        qlist = [int(v) for v in os.environ['QLIST'].split(',')] if os.environ.get('QLIST') else range(NQ)
        for i in qlist:
            qtile(i)
        if stop is not None:
            for e in S_.ENGS:
                S_.ops[e] = S_.ops[e][:marks[stop][e]]
        S_.emit()
    return nc


def prep_nsa(inp, b, g, S):
    H, G_, Dh = 16, 4, 64
    w_in = inp["nsa_w_in"][0]
    qc = w_in[:, g * 256:(g + 1) * 256]
    kvo = H * Dh
    def kvcol(i):
        return w_in[:, kvo + (i * G_ + g) * 64: kvo + (i * G_ + g + 1) * 64]
    kc, vc, ks, vs, kw, vw = [kvcol(i) for i in range(6)]
    go = kvo + 6 * G_ * Dh
    gates = w_in[:, go + g * 12: go + (g + 1) * 12]
    d = {}
    d["x"] = np.ascontiguousarray(inp["x"][b, :S])
    d["gmix"] = inp["norm_mix"][0]
    d["w_tok1"] = np.ascontiguousarray(np.concatenate([qc, ks, ks, kw, kw], axis=1))
    d["w_tok2"] = np.ascontiguousarray(np.concatenate([vs, vw, gates], axis=1))
    d["w_feat"] = np.ascontiguousarray(np.concatenate([kc, kc, vc, vc], axis=1))
    qg = inp["nsa_q_gain"][0]
    kg = inp["nsa_k_gain"][0]
    d["gain1"] = np.concatenate([qg, qg, qg, qg, kg[1], kg[1], kg[2], kg[2]])[None].astype(np.float32)
    d["cpos"] = np.ascontiguousarray(inp["nsa_cmp_pos"][0].reshape(2, 2048))
    d["cw1"] = np.ascontiguousarray(inp["nsa_cmp_w1"][0])
    w2 = inp["nsa_cmp_w2"][0]
    d["cw2k"] = np.ascontiguousarray(np.concatenate([w2[0], w2[0]], axis=1))
    d["cw2v"] = np.ascontiguousarray(w2[1])
    d["kg0"] = np.concatenate([kg[0], kg[0]])[None].astype(np.float32)
    d["relg"] = np.ascontiguousarray(inp["rel_table"][:, g * 4:(g + 1) * 4])
    return d


def moba_consts():
    c = {}
    c["idn"] = np.eye(128, dtype=np.float32)
    c["jmat"] = np.eye(128, dtype=np.float32)[::-1].copy().astype(BF)
    oh_s, _ = bias_onehots()
    c["oh_s"] = oh_s
    return c


def build_moba(S):
    NQ = S // 128
    NB = S // 256
    NBP = max(NB, 8)
    L = NOFF * 128 + 256
    nc = bass.Bass("TRN2", target_bir_lowering=False)
    D = nc.dram_tensor

    def inp(name, shape, dt=F32):
        return D(name, shape, dt, kind="ExternalInput").ap()
    x = inp("x", [S, 1024])
    gmix = inp("gmix", [1024])
    gkv = inp("gkv", [1024])
    w1d = inp("w_tok1", [1024, 384])
    w2d = inp("w_tok2", [1024, 64])
    gain1d = inp("gain1", [1, 384])
    reld = inp("relg", [32, 4])
    idn = inp("idn", [128, 128])
    jmatd = inp("jmat", [128, 128], BF16)
    ohsd = inp("oh_s", [33, L])
    oout = D("o", [S, 256], F32, kind="ExternalOutput").ap()
    qtd = D("qtd", [NQ, 128, 256], BF16, kind="Internal").ap()
    fvd = D("fvd", [4, L], F32, kind="Internal").ap()

    with contextlib.ExitStack() as st:
        NM = {}

        def T(name, shape, dt):
            t = st.enter_context(nc.sbuf_tensor("s_" + name, shape, dt))
            NM[id(t)] = name
            return t
        ps = st.enter_context(nc.psum_tensor("ps", [128, 8, 512], F32))
        psb = ps[:].bitcast(BF16)
        S_ = Sched(nc)
        op = S_.op
        dma = S_.dma

        def bk(i):
            return f"bank{i}"

        idf = T("idf", [128, 128], F32)
        idb = T("idb", [128, 128], BF16)
        jmat = T("jmat", [128, 128], BF16)
        i_s = T("i_s", [128, 128], BF16)
        eps = T("eps", [128, 1], F32)
        gn = T("gn", [128, 8], F32)
        gn2 = T("gn2", [128, 8], F32)
        wt1 = T("wt1", [128, 8, 384], BF16)
        wt2 = T("wt2", [128, 8, 64], BF16)
        gain1 = T("gain1", [128, 384], F32)
        KST = T("KST", [128, S], BF16)
        VS = T("VS", [128, NQ, 80], BF16)
        kmT = T("kmT", [128, NBP], F32)
        kmTb = T("kmTb", [128, NBP], BF16)
        bias_s = T("bias_s", [128, NOFF, 4, 128], BF16)
        stg = [T(f"stg{i}", [128, 1024], F32) for i in range(2)]

        dma("sp", idf[:], idn[:, :], writes=["idf"])
        dma("sp", jmat[:], jmatd[:, :], writes=["jmat"])
        dma("sp", gn[:], gmix.rearrange("(c p) -> p c", p=128), writes=["gn"], allow_slow_non_contiguous=True)
        dma("sp", gn2[:], gkv.rearrange("(c p) -> p c", p=128), writes=["gn2"], allow_slow_non_contiguous=True)
        dma("sp", gain1[:], gain1d.partition_broadcast(128), writes=["gain1"])
        op("dve", lambda e: e.memset(eps[:], 1e-6), writes=["eps"])
        op("dve", lambda e: e.tensor_copy(out=idb[:], in_=idf[:]), reads=["idf"], writes=["idb"])
        op("dve", lambda e: e.tensor_scalar(out=i_s[:], in0=idf[:], scalar1=-NEGM, scalar2=None, op0=ALU.mult), reads=["idf"], writes=["i_s"])
        op("dve", lambda e: e.tensor_scalar(out=gain1[:, 0:256], in0=gain1[:, 0:256], scalar1=0.125, scalar2=None, op0=ALU.mult), reads=["gain1"], writes=["gain1"])
        op("pool", lambda e: e.memset(VS[:, :, 64:65], 1.0), writes=["VS"])
        op("pool", lambda e: e.memset(kmT[:], 0.0), writes=["kmT"])

        for c in range(8):
            b = c % 2
            dma("sp", stg[b][:, 0:384], w1d[c * 128:(c + 1) * 128, :], writes=[f"stg{b}"])
            dma("sp", stg[b][:, 384:448], w2d[c * 128:(c + 1) * 128, :], writes=[f"stg{b}"])
            op("dve", lambda e, c=c, b=b: e.tensor_scalar(out=wt1[:, c, 0:256], in0=stg[b][:, 0:256], scalar1=gn[:, c:c + 1], scalar2=None, op0=ALU.mult),
               reads=[f"stg{b}", "gn"], writes=["wt1"])
            op("dve", lambda e, c=c, b=b: e.tensor_scalar(out=wt1[:, c, 256:384], in0=stg[b][:, 256:384], scalar1=gn2[:, c:c + 1], scalar2=None, op0=ALU.mult),
               reads=[f"stg{b}", "gn2"], writes=["wt1"])
            op("dve", lambda e, c=c, b=b: e.tensor_scalar(out=wt2[:, c, :], in0=stg[b][:, 384:448], scalar1=gn2[:, c:c + 1], scalar2=None, op0=ALU.mult),
               reads=[f"stg{b}", "gn2"], writes=["wt2"])

        relt = T("relt", [33, 4], F32)
        oh = T("oh", [33, L], F32)
        fv = T("fv", [4, L], F32)
        op("dve", lambda e: e.memset(relt[:], 1.0), writes=["relt"])
        dma("sp", relt[0:32, :], reld[:, :], reads=[], writes=["relt"])
        dma("sp", oh[:], ohsd[:, :], writes=["oh"])
        for c0 in range(0, L, 512):
            c1 = min(L, c0 + 512)
            op("pe", lambda e, c0=c0, c1=c1: e.matmul(ps[0:4, 0, 0:c1 - c0], lhsT=relt[:, :], rhs=oh[:, c0:c1], start=True, stop=True),
               reads=["relt", "oh"], writes=[bk(0)])
            op("dve", lambda e, c0=c0, c1=c1: e.tensor_copy(out=fv[:, c0:c1], in_=ps[0:4, 0, 0:c1 - c0]), reads=[bk(0)], writes=["fv"])
        dma("sp", fvd[:, :], fv[:], reads=["fv"], writes=["fvd"])
        for h in range(4):
            b = h % 2
            w = NOFF * 128
            for c0 in range(0, w, 1024):
                c1 = min(w, c0 + 1024)
                dma("sp", stg[b][:, 0:c1 - c0], bass.AP(fvd.tensor, fvd[h, c0:c0 + 1].offset, [[1, 128], [1, c1 - c0]]),
                    reads=["fvd"], writes=[f"stg{b}"])
                bi = HORD.index(h)
                op("dve", lambda e, b=b, c0=c0, c1=c1, bi=bi: e.tensor_copy(
                    out=bias_s[:, c0 // 128:c1 // 128, bi, :], in_=stg[b][:, 0:c1 - c0].rearrange("p (o q) -> p o q", q=128)),
                   reads=[f"stg{b}"], writes=["bias_s"])

        xt = [T(f"xt{i}", [128, 1024], F32) for i in range(2)]
        sq = T("sq", [128, 1024], BF16)
        ss = T("ss", [128, 1], F32)
        rs = T("rs", [128, 1], F32)
        xn = T("xn", [128, 1024], BF16)
        hnT = T("hnT", [128, 8, 128], BF16)
        sq1 = T("sq1", [128, 384], F32)
        ssq = T("ssq", [128, 6], F32)
        rq = T("rq", [128, 6], F32)
        t1 = T("t1", [128, 384], F32)
        qk = [T(f"qk{i}", [128, 384], BF16) for i in range(2)]
        qTt = [T(f"qTt{i}", [128, 256], BF16) for i in range(2)]

        def P1a(i):
            b = i % 2
            dma("sp", xt[b][:], x[i * 128:(i + 1) * 128, :], writes=[f"xt{b}"])
            op("act", lambda e: e.activation(out=sq[:], in_=xt[b][:], func=AF.Square, accum_out=ss[:]), reads=[f"xt{b}"], writes=["sq", "ss"])
            op("act", lambda e: e.activation(out=rs[:], in_=ss[:], func=AF.Sqrt, bias=eps[:], scale=1.0 / 1024), reads=["ss", "eps"], writes=["rs"])
            op("dve", lambda e: e.reciprocal(out=rs[:], in_=rs[:]), reads=["rs"], writes=["rs"])
            op("dve", lambda e: e.tensor_scalar(out=xn[:], in0=xt[b][:], scalar1=rs[:], scalar2=None, op0=ALU.mult), reads=[f"xt{b}", "rs"], writes=["xn"])
            for c in range(8):
                op("pe", lambda e, c=c: e.transpose(out=psb[:, 0, c * 128:(c + 1) * 128], in_=xn[:, c * 128:(c + 1) * 128], identity=idb[:]),
                   reads=["xn", "idb"], writes=[bk(0)])
            op("act", lambda e: e.copy(out=hnT[:].rearrange("p a b -> p (a b)"), in_=psb[:, 0, 0:1024]), reads=[bk(0)], writes=["hnT"])
            for c in range(8):
                op("pe", lambda e, c=c: e.matmul(ps[:, 1, 0:384], lhsT=hnT[:, c, :], rhs=wt1[:, c, :], start=(c == 0), stop=(c == 7)),
                   reads=["hnT", "wt1"], writes=[bk(1)])
            for c in range(8):
                op("pe", lambda e, c=c: e.matmul(ps[:, 2, 0:64], lhsT=hnT[:, c, :], rhs=wt2[:, c, :], start=(c == 0), stop=(c == 7)),
                   reads=["hnT", "wt2"], writes=[bk(2)])
            op("act", lambda e: e.activation(out=sq1[:], in_=ps[:, 1, 0:384], func=AF.Square), reads=[bk(1)], writes=["sq1"])
            op("dve", lambda e: e.tensor_reduce(out=ssq[:], in_=sq1[:].rearrange("p (a d) -> p a d", d=64), axis=AX.X, op=ALU.add), reads=["sq1"], writes=["ssq"])
            op("act", lambda e: e.activation(out=rq[:], in_=ssq[:], func=AF.Sqrt, bias=eps[:], scale=1.0 / 64), reads=["ssq", "eps"], writes=["rq"])
            op("dve", lambda e: e.reciprocal(out=rq[:], in_=rq[:]), reads=["rq"], writes=["rq"])
            op("dve", lambda e: e.tensor_tensor(out=t1[:].rearrange("p (a d) -> p a d", d=64), in0=ps[:, 1, 0:384].rearrange("p (a d) -> p a d", d=64),
                                                in1=rq[:].unsqueeze(2).broadcast_to([128, 6, 64]), op=ALU.mult), reads=[bk(1), "rq"], writes=["t1"])
            op("pool", lambda e: e.tensor_tensor(out=qk[b][:], in0=t1[:], in1=gain1[:], op=ALU.mult), reads=["t1", "gain1"], writes=[f"qk{b}"])
            op("dve", lambda e: e.tensor_copy(out=VS[:, i, 0:64], in_=ps[:, 2, 0:64]), reads=[bk(2)], writes=["VS"])

        def P1b(i):
            b = i % 2
            for j in range(3):
                op("pe", lambda e, j=j: e.transpose(out=psb[:, 4, j * 128:(j + 1) * 128], in_=qk[b][:, j * 128:(j + 1) * 128], identity=idb[:]),
                   reads=[f"qk{b}", "idb"], writes=[bk(4)])
            op("act", lambda e: e.copy(out=qTt[b][:], in_=psb[:, 4, 0:256]), reads=[bk(4)], writes=[f"qTt{b}"])
            op("dve", lambda e: e.tensor_copy(out=KST[:, i * 128:(i + 1) * 128], in_=psb[:, 4, 256:384]), reads=[bk(4)], writes=["KST"])
            dma("sp", qtd[i], qTt[b][:], reads=[f"qTt{b}"], writes=["qtd"])

        P1a(0)
        for i in range(NQ):
            if i + 1 < NQ:
                P1a(i + 1)
            P1b(i)

        op("dve", lambda e: e.tensor_reduce(out=kmT[:, 0:NB], in_=KST[:].rearrange("p (n k) -> p n k", k=256), axis=AX.X, op=ALU.add), reads=["KST"], writes=["kmT"])
        op("dve", lambda e: e.tensor_scalar(out=kmTb[:], in0=kmT[:], scalar1=1.0 / 256, scalar2=None, op0=ALU.mult), reads=["kmT"], writes=["kmTb"])

        qT = [T(f"qT{i}", [128, 2, 128], BF16) for i in range(2)]
        PT = [T(f"PT{i}", [128, 512], BF16) for i in range(3)]
        Oe = T("Oe", [65, 512], F32)
        den = T("den", [128, 4], F32)
        oacc = [T(f"oacc{i}", [128, 256], F32) for i in range(2)]
        score = T("score", [128, 4, NBP], F32)
        m8 = T("m8", [128, 4, 8], F32)
        nsel = T("nsel", [128, 4, NBP], BF16)
        NX = T("NX", [128, 4, NBP, 128], BF16)
        cnt = {"st": 0, "pt": 0}

        def qtile(i):
            qb = i % 2
            cblk = i // 2
            dma("sp", qT[qb][:].rearrange("p a b -> p (a b)"), qtd[i], reads=["qtd"], writes=[f"qT{qb}"])
            if cblk > 0:
                for half in range(2):
                    rows = slice(64 * half, 64 * half + 64)
                    gb = 5 if half == 0 else 7
                    for c in range(2):
                        op("pe", lambda e, rows=rows, gb=gb, c=c: e.matmul(ps[:, gb, c * NBP:(c + 1) * NBP], lhsT=qT[qb][rows, c, :], rhs=kmTb[rows, :],
                                                                          start=True, stop=True), reads=[f"qT{qb}", "kmTb"], writes=[bk(gb)])
                op("pool", lambda e: e.memset(score[:], -1e9), writes=["score"])
                op("dve", lambda e: e.tensor_copy(out=score[:, 0:2, 0:cblk], in_=ps[:, 5, 0:2 * NBP].rearrange("p (c n) -> p c n", n=NBP)[:, :, 0:cblk]),
                   reads=[bk(5)], writes=["score"])
                op("dve", lambda e: e.tensor_copy(out=score[:, 2:4, 0:cblk], in_=ps[:, 7, 0:2 * NBP].rearrange("p (c n) -> p c n", n=NBP)[:, :, 0:cblk]),
                   reads=[bk(7)], writes=["score"])
                for bi in range(4):
                    op("dve", lambda e, bi=bi: e.max(out=m8[:, bi, :], in_=score[:, bi, :]), reads=["score"], writes=["m8"])
                    op("dve", lambda e, bi=bi: e.tensor_scalar(out=nsel[:, bi, :], in0=score[:, bi, :], scalar1=m8[:, bi, 2:3], scalar2=1.0,
                                                              op0=ALU.is_ge, op1=ALU.subtract), reads=["score", "m8"], writes=["nsel"])
                op("pool", lambda e: e.tensor_copy(out=NX[:, :, 0:cblk, :], in_=nsel[:, :, 0:cblk].unsqueeze(3).broadcast_to([128, 4, cblk, 128])),
                   reads=["nsel"], writes=["NX"])
            nt = i + 1
            for kt in range(i + 1):
                o_ = i - kt
                n = kt // 2
                past = n < cblk
                sb = 2 * (cnt["st"] % 2)
                cnt["st"] += 1
                pi = cnt["pt"] % 3
                cnt["pt"] += 1
                hasb = o_ < NOFF
                for half in range(2):
                    rows = slice(64 * half, 64 * half + 64)
                    op("pe", lambda e, half=half, rows=rows, sb=sb, kt=kt, last=(not hasb and not past): e.matmul(
                        ps[:, sb + half, 0:256], lhsT=KST[rows, kt * 128:(kt + 1) * 128], rhs=qT[qb][rows, :, :], start=True, stop=last),
                       reads=["KST", f"qT{qb}"], writes=[bk(sb + half)])
                if hasb:
                    for half in range(2):
                        op("pe", lambda e, half=half, sb=sb, o_=o_, last=(not past): e.matmul(
                            ps[:, sb + half, 0:256], lhsT=jmat[:, :], rhs=bias_s[:, o_, 2 * half:2 * half + 2, :], start=False, stop=last),
                           reads=["jmat", "bias_s"], writes=[bk(sb + half)])
                if past:
                    for half in range(2):
                        for c in range(2):
                            bi = 2 * half + c
                            op("pe", lambda e, half=half, sb=sb, c=c, bi=bi, n=n: e.matmul(
                                ps[:, sb + half, c * 128:(c + 1) * 128], lhsT=NX[:, bi, n, :], rhs=i_s[:, :], start=False, stop=(c == 1)),
                               reads=["NX", "i_s"], writes=[bk(sb + half)])
                op("act", lambda e, sb=sb, pi=pi: e.activation(out=PT[pi][:].rearrange("p (a b) -> p a b", a=2), in_=ps[:, sb:sb + 2, 0:256], func=AF.Exp),
                   reads=[bk(sb), bk(sb + 1)], writes=[f"PT{pi}"])
                op("pe", lambda e, pi=pi, kt=kt: e.matmul(ps[0:65, 4, :], lhsT=VS[:, kt, 0:65], rhs=PT[pi][:], start=(kt == 0), stop=(kt == nt - 1)),
                   reads=[f"PT{pi}", "VS"], writes=[bk(4)])
            ob = i % 2
            op("act", lambda e: e.copy(out=Oe[:], in_=ps[0:65, 4, :]), reads=[bk(4)], writes=["Oe"])
            for bi in range(4):
                op("pe", lambda e, bi=bi: e.transpose(out=ps[:, 6, bi * 68:bi * 68 + 65], in_=Oe[0:65, bi * 128:(bi + 1) * 128], identity=idf[0:65, 0:65]),
                   reads=["Oe", "idf"], writes=[bk(6)])
            pv = ps[:, 6, 0:272].rearrange("p (a d) -> p a d", d=68)
            op("dve", lambda e: e.tensor_scalar(out=den[:], in0=pv[:, :, 64], scalar1=1e-30, scalar2=None, op0=ALU.max), reads=[bk(6)], writes=["den"])
            op("dve", lambda e: e.reciprocal(out=den[:], in_=den[:]), reads=["den"], writes=["den"])
            for bi in range(4):
                h = HORD[bi]
                op("dve", lambda e, bi=bi, h=h: e.tensor_scalar(out=oacc[ob][:, h * 64:(h + 1) * 64], in0=pv[:, bi, 0:64], scalar1=den[:, bi:bi + 1],
                                                                scalar2=None, op0=ALU.mult), reads=[bk(6), "den"], writes=[f"oacc{ob}"])
            dma("sp", oout[i * 128:(i + 1) * 128, :], oacc[ob][:], reads=[f"oacc{ob}"])

        for i in range(NQ):
            qtile(i)
        S_.emit()
    return nc


def prep_moba(inp, h1b, g, S):
    d = {}
    d["x"] = np.ascontiguousarray(h1b[:S])
    d["gmix"] = inp["norm_mix"][1]
    d["gkv"] = inp["kv_norm"]
    wq = inp["moba_w_q"][0][:, g * 256:(g + 1) * 256]
    kw = inp["kv_w"][:, g * 64:(g + 1) * 64]
    vw = inp["kv_w"][:, 256 + g * 64:256 + (g + 1) * 64]
    d["w_tok1"] = np.ascontiguousarray(np.concatenate([wq, kw, kw], axis=1))
    d["w_tok2"] = np.ascontiguousarray(vw)
    qg = inp["moba_q_gain"][0]
    kg = inp["kv_k_gain"]
    d["gain1"] = np.concatenate([qg, qg, qg, qg, kg, kg])[None].astype(np.float32)
    d["relg"] = np.ascontiguousarray(inp["rel_table"][:, g * 4:(g + 1) * 4])
    return d


NT = 2048

def build_mlp(ntok=NT):
    nc = bass.Bass("TRN2", target_bir_lowering=False)
    D = nc.dram_tensor
    x = D("x", [ntok, 1024], F32, kind="ExternalInput").ap()
    o = D("o", [ntok, 1024], F32, kind="ExternalInput").ap()
    w_out = D("w_out", [1024, 1024], F32, kind="ExternalInput").ap()
    gmlp = D("gmlp", [1024], F32, kind="ExternalInput").ap()
    w_up = D("w_up", [1024, 4096], F32, kind="ExternalInput").ap()
    w_dn = D("w_dn", [4096, 1024], F32, kind="ExternalInput").ap()
    idn = D("idn", [128, 128], F32, kind="ExternalInput").ap()
    y = D("y", [ntok, 1024], F32, kind="ExternalOutput").ap()
    ntile = ntok // 128
    with contextlib.ExitStack() as st:
        T = lambda name, shape, dt: st.enter_context(nc.sbuf_tensor(name, shape, dt))
        P = lambda name, shape, dt: st.enter_context(nc.psum_tensor(name, shape, dt))
        S = Sched(nc)
        wout = T("wout", [128, 8, 1024], BF16)
        wup = T("wup", [128, 8, 4096], BF16)
        wdn = T("wdn", [128, 32, 1024], BF16)
        gn = T("gn", [128, 8], F32)
        idf = T("idf", [128, 128], F32)
        idb = T("idb", [128, 128], BF16)
        eps = T("eps", [128, 1], F32)
        xt = [T(f"xt{i}", [128, 1024], F32) for i in range(2)]
        ot = [T(f"ot{i}", [128, 1024], F32) for i in range(2)]
        ob = T("ob", [128, 1024], BF16)
        oT = T("oT", [128, 8, 128], BF16)
        h = [T(f"h{i}", [128, 1024], F32) for i in range(2)]
        sq = T("sq", [128, 1024], BF16)
        ss = T("ss", [128, 1], F32)
        rs = T("rs", [128, 1], F32)
        hn = T("hn", [128, 1024], BF16)
        hnT = [T(f"hnT{i}", [128, 8, 128], BF16) for i in range(2)]
        a2T = [T(f"a2T{i}", [128, 32, 128], BF16) for i in range(2)]
        tmp = [T(f"tmp{i}", [128, 512], F32) for i in range(2)]
        yt = [T(f"yt{i}", [128, 512], F32) for i in range(2)]
        pT = [P(f"pT{i}", [128, 8, 128], BF16) for i in range(2)]
        py = [P(f"py{i}", [128, 512], F32) for i in range(2)]
        pa = [P(f"pa{i}", [128, 512], F32) for i in range(2)]
        po = [P(f"po{i}", [128, 512], F32) for i in range(2)]

        S.dma("sp", idf[:], idn[:, :], writes=["idf"])
        S.dma("sp", gn[:], gmlp.rearrange("(c p) -> p c", p=128), writes=["gn"], allow_slow_non_contiguous=True)
        S.op("dve", lambda e: e.memset(eps[:], 1e-6), writes=["eps"])
        S.op("dve", lambda e: e.tensor_copy(out=idb[:], in_=idf[:]), reads=["idf"], writes=["idb"])
        stg = [xt[0], xt[1], ot[0], ot[1]]
        stk = ["xt0", "xt1", "ot0", "ot1"]
        si = [0]
        ceng = ["pool", "dve", "act"]

        def stage_cast(src_ap, dst_ap, dkey, scalar=None):
            i = si[0] % 4
            e2 = ceng[si[0] % 2]
            si[0] += 1
            S.dma("sp", stg[i][:], src_ap, writes=[stk[i]])
            if scalar is None:
                S.op(e2, lambda e: e.tensor_copy(out=dst_ap, in_=stg[i][:]), reads=[stk[i]], writes=[dkey])
            else:
                S.op(e2, lambda e: e.tensor_scalar(out=dst_ap, in0=stg[i][:], scalar1=scalar, scalar2=None, op0=ALU.mult),
                     reads=[stk[i], "gn"], writes=[dkey])
        for c in range(8):
            stage_cast(w_out[c * 128:(c + 1) * 128, :], wout[:, c, :], "wout")
        for c in range(8):
            for q in range(4):
                stage_cast(w_up[c * 128:(c + 1) * 128, q * 1024:(q + 1) * 1024], wup[:, c, q * 1024:(q + 1) * 1024], "wup", scalar=gn[:, c:c + 1])
        for f in range(32):
            stage_cast(w_dn[f * 128:(f + 1) * 128, :], wdn[:, f, :], "wdn")

        def F1(i):
            b = i % 2
            S.dma("sp", xt[b][:], x[i * 128:(i + 1) * 128, :], writes=[f"xt{b}"])
            S.dma("sp", ot[b][:], o[i * 128:(i + 1) * 128, :], writes=[f"ot{b}"])
            S.op("pool", lambda e: e.tensor_copy(out=ob[:], in_=ot[b][:]), reads=[f"ot{b}"], writes=["ob"])
            for c in range(8):
                S.op("pe", lambda e, c=c: e.transpose(out=pT[0][:, c, :], in_=ob[:, c * 128:(c + 1) * 128], identity=idb[:]),
                     reads=["ob", "idb"], writes=["pT0"])
            S.op("act", lambda e: e.copy(out=oT[:], in_=pT[0][:]), reads=["pT0"], writes=["oT"])
            for hf in range(2):
                for c in range(8):
                    S.op("pe", lambda e, c=c, hf=hf: e.matmul(py[hf][:], lhsT=oT[:, c, :], rhs=wout[:, c, hf * 512:(hf + 1) * 512],
                                                              start=(c == 0), stop=(c == 7)), reads=["oT", "wout"], writes=[f"py{hf}"])
                S.op("dve", lambda e, hf=hf: e.tensor_tensor(out=h[b][:, hf * 512:(hf + 1) * 512], in0=xt[b][:, hf * 512:(hf + 1) * 512],
                                                               in1=py[hf][:], op=ALU.add), reads=[f"xt{b}", f"py{hf}"], writes=[f"h{b}"])
            S.op("act", lambda e: e.activation(out=sq[:], in_=h[b][:], func=AF.Square, accum_out=ss[:]), reads=[f"h{b}"], writes=["sq", "ss"])
            S.op("act", lambda e: e.activation(out=rs[:], in_=ss[:], func=AF.Sqrt, bias=eps[:], scale=1.0 / 1024), reads=["ss", "eps"], writes=["rs"])
            S.op("dve", lambda e: e.reciprocal(out=rs[:], in_=rs[:]), reads=["rs"], writes=["rs"])
            S.op("dve", lambda e: e.tensor_scalar(out=hn[:], in0=h[b][:], scalar1=rs[:], scalar2=None, op0=ALU.mult),
                 reads=[f"h{b}", "rs"], writes=["hn"])

        def F2(i):
            b = i % 2
            for c in range(8):
                S.op("pe", lambda e, c=c: e.transpose(out=pT[1][:, c, :], in_=hn[:, c * 128:(c + 1) * 128], identity=idb[:]),
                     reads=["hn", "idb"], writes=["pT1"])
            S.op("act", lambda e: e.copy(out=hnT[b][:], in_=pT[1][:]), reads=["pT1"], writes=[f"hnT{b}"])

        def U(i):
            b = i % 2
            for fg in range(8):
                pb = fg % 2
                for fi in range(4):
                    f = fg * 4 + fi
                    for c in range(8):
                        S.op("pe", lambda e, c=c, f=f, fi=fi, pb=pb: e.matmul(pa[pb][:, fi * 128:(fi + 1) * 128], lhsT=wup[:, c, f * 128:(f + 1) * 128],
                                                                             rhs=hnT[b][:, c, :], start=(c == 0), stop=(c == 7)),
                             reads=[f"hnT{b}", "wup"], writes=[f"pa{pb}"])
                S.op("act", lambda e, pb=pb: e.activation(out=tmp[pb][:], in_=pa[pb][:], func=AF.Relu), reads=[f"pa{pb}"], writes=[f"tmp{pb}"])
                S.op("pool", lambda e, pb=pb, fg=fg: e.tensor_tensor(out=a2T[b][:, fg * 4:(fg + 1) * 4, :], in0=tmp[pb][:].rearrange("p (a t) -> p a t", a=4),
                                                                     in1=tmp[pb][:].rearrange("p (a t) -> p a t", a=4), op=ALU.mult),
                     reads=[f"tmp{pb}"], writes=[f"a2T{b}"])

        def Dn(i):
            b = i % 2
            for hf in range(2):
                for f in range(32):
                    S.op("pe", lambda e, f=f, hf=hf: e.matmul(po[hf][:], lhsT=a2T[b][:, f, :], rhs=wdn[:, f, hf * 512:(hf + 1) * 512],
                                                              start=(f == 0), stop=(f == 31)), reads=[f"a2T{b}", "wdn"], writes=[f"po{hf}"])
                S.op("dve", lambda e, hf=hf: e.tensor_tensor(out=yt[hf][:], in0=h[b][:, hf * 512:(hf + 1) * 512], in1=po[hf][:], op=ALU.add),
                     reads=[f"h{b}", f"po{hf}"], writes=[f"yt{hf}"])
                S.dma("sp", y[i * 128:(i + 1) * 128, hf * 512:(hf + 1) * 512], yt[hf][:], reads=[f"yt{hf}"])

        F1(0); F2(0)
        for i in range(ntile):
            if i + 1 < ntile:
                F1(i + 1)
            U(i)
            if i + 1 < ntile:
                F2(i + 1)
            Dn(i)
        S.emit()
    return nc


def kernel(**inputs):
    inp = {k: np.ascontiguousarray(np.asarray(v), dtype=np.float32) for k, v in inputs.items()}
    S = 8192
    cores = list(range(8))
    eye = np.eye(128, dtype=np.float32)
    C = np.ascontiguousarray
    ncA = build_nsa(S)
    cs = nsa_consts(S)
    maps = []
    for b in range(2):
        for g in range(4):
            d = prep_nsa(inp, b, g, S)
            d.update(cs)
            maps.append(d)
    rA = run_bass_kernel_spmd(ncA, maps, core_ids=cores).results
    o0 = np.empty((2, S, 1024), np.float32)
    for b in range(2):
        for g in range(4):
            o0[b, :, g * 256:(g + 1) * 256] = rA[b * 4 + g]["o"]
    xf = inp["x"].reshape(2 * S, 1024)
    of = o0.reshape(2 * S, 1024)
    ncD = build_mlp(2048)
    maps = [dict(x=C(xf[c * 2048:(c + 1) * 2048]), o=C(of[c * 2048:(c + 1) * 2048]), w_out=inp["nsa_w_out"][0], gmlp=inp["norm_mlp"][0],
                 w_up=inp["mlp_w_up"][0], w_dn=inp["mlp_w_down"][0], idn=eye) for c in range(8)]
    rD = run_bass_kernel_spmd(ncD, maps, core_ids=cores).results
    h1 = np.concatenate([r["y"] for r in rD], axis=0).reshape(2, S, 1024)
    ncC = build_moba(S)
    cm = moba_consts()
    maps = []
    for b in range(2):
        for g in range(4):
            d = prep_moba(inp, h1[b], g, S)
            d.update(cm)
            maps.append(d)
    rC = run_bass_kernel_spmd(ncC, maps, core_ids=cores).results
    o1 = np.empty((2, S, 1024), np.float32)
    for b in range(2):
        for g in range(4):
            o1[b, :, g * 256:(g + 1) * 256] = rC[b * 4 + g]["o"]
    hf = h1.reshape(2 * S, 1024)
    of = o1.reshape(2 * S, 1024)
    ncD2 = build_mlp(2048)
    maps = [dict(x=C(hf[c * 2048:(c + 1) * 2048]), o=C(of[c * 2048:(c + 1) * 2048]), w_out=inp["moba_w_out"][0], gmlp=inp["norm_mlp"][1],
                 w_up=inp["mlp_w_up"][1], w_dn=inp["mlp_w_down"][1], idn=eye) for c in range(8)]
    rD2 = run_bass_kernel_spmd(ncD2, maps, core_ids=cores).results
    out = np.concatenate([r["y"] for r in rD2], axis=0).reshape(2, S, 1024)
    return out.astype(np.float32)
```
